# Optimizing a Trainium2 kernel written in Bass

```python
import math
import jax, jax.numpy as jnp
from jax import lax
import numpy as np

D_MODEL = 1024
BATCH = 1
SEQ = 16384
DEPTH = 2
DEC_BATCH = 16
DEC_SEQ = 4096
PAST_LEN = 128

MIX_WIDTH = D_MODEL
GROUP_WIDTH = MIX_WIDTH // 4
CONV_WIDTH = 3
CONV_CH = GROUP_WIDTH
DIFF_HEADS = 4
DIFF_HEAD_DIM = GROUP_WIDTH // DIFF_HEADS
DIFF_QK_DIM = DIFF_HEAD_DIM // 2
MLA_HEADS = 4
MLA_NOPE = 64
MLA_ROPE = 32
MLA_V = GROUP_WIDTH // MLA_HEADS
MLA_Q_RANK = 256
MLA_KV_RANK = 128
ROPE_THETA = 10000.0
FOURIER_CH = GROUP_WIDTH
FOURIER_GROUPS = 4
D_FF = 2816
Q_BLOCK = 128
NORM_EPS = 1e-6
N_MOD = 9

_SEG = (CONV_CH, CONV_CH, CONV_CH,
        GROUP_WIDTH, GROUP_WIDTH, GROUP_WIDTH,
        MLA_Q_RANK, MLA_KV_RANK, MLA_ROPE,
        FOURIER_CH)
IN_COLS = sum(_SEG)
SPLIT_POINTS = tuple(int(v) for v in np.cumsum(_SEG)[:-1])

kernel_name = "hybrid_parallel_encoder"


def _rms(x, g):
    xf = x.astype(jnp.float32)
    y = xf * lax.rsqrt(jnp.mean(xf * xf, axis=-1, keepdims=True) + NORM_EPS)
    return (y * g.astype(jnp.float32)).astype(x.dtype)


def _alibi_slopes(n):
    return jnp.asarray(2.0 ** (-8.0 * np.arange(1, n + 1) / n), dtype=jnp.float32)


def _query_blocks(t):
    b, s, h, d = t.shape
    return jnp.moveaxis(t.reshape(b, s // Q_BLOCK, Q_BLOCK, h, d), 1, 0)


def _unblock(t):
    n, b, q, h, d = t.shape
    return jnp.moveaxis(t, 0, 1).reshape(b, n * q, h, d)


def _swiglu(h, w_gu, w_down):
    gu = jnp.einsum('bsd,df->bsf', h, w_gu)
    g, u = jnp.split(gu, 2, axis=-1)
    return jnp.einsum('bsf,fd->bsd', jax.nn.silu(g) * u, w_down)


def _rope(t):
    s = t.shape[1]
    inv = ROPE_THETA ** (-jnp.arange(0, MLA_ROPE, 2, dtype=jnp.float32) / MLA_ROPE)
    ang = jnp.arange(s, dtype=jnp.float32)[:, None] * inv[None, :]
    cos = jnp.cos(ang)[None, :, None, :]
    sin = jnp.sin(ang)[None, :, None, :]
    t1, t2 = jnp.split(t.astype(jnp.float32), 2, axis=-1)
    return jnp.concatenate([t1 * cos - t2 * sin, t1 * sin + t2 * cos], axis=-1).astype(t.dtype)


def _diff_attention(q1, q2, k1, k2, v, lam, slopes):
    b, s, h, dq = q1.shape
    scale = dq ** -0.5
    k_pos = jnp.arange(s, dtype=jnp.float32)
    vf = v.astype(jnp.float32)
    n_blk = s // Q_BLOCK

    def block(args):
        i, q1b, q2b = args
        q_pos = (i * Q_BLOCK + jnp.arange(Q_BLOCK)).astype(jnp.float32)
        bias = -slopes[:, None, None] * jnp.abs(q_pos[:, None] - k_pos[None, :])[None]
        s1 = jnp.einsum('bqhd,bkhd->bhqk', q1b, k1, preferred_element_type=jnp.float32) * scale + bias
        s2 = jnp.einsum('bqhd,bkhd->bhqk', q2b, k2, preferred_element_type=jnp.float32) * scale + bias
        a = jax.nn.softmax(s1, axis=-1) - lam * jax.nn.softmax(s2, axis=-1)
        return jnp.einsum('bhqk,bkhd->bqhd', a, vf)

    out = lax.map(block, (jnp.arange(n_blk), _query_blocks(q1), _query_blocks(q2)))
    return _unblock(out)


def _mla_attention(q, k, v):
    scale = q.shape[-1] ** -0.5
    vf = v.astype(jnp.float32)

    def block(qb):
        sc = jnp.einsum('bqhd,bkhd->bhqk', qb, k, preferred_element_type=jnp.float32) * scale
        p = jax.nn.softmax(sc, axis=-1)
        return jnp.einsum('bhqk,bkhd->bqhd', p, vf)

    return _unblock(lax.map(block, _query_blocks(q)))


def _token_mixer(h, p, l):
    b, s, _ = h.shape
    dt = h.dtype
    proj = jnp.einsum('bsd,de->bse', h, p['w_in'][l])
    (a_b, a_c, a_x, d_q, d_k, d_v, m_cq, m_ckv, m_kpe, f_u) = jnp.split(proj, SPLIT_POINTS, axis=-1)

    conv_w = p['conv_w'][l]
    u = a_c * a_x
    up = jnp.pad(u, ((0, 0), (1, 1), (0, 0)))
    conv = up[:, :-2] * conv_w[0] + up[:, 1:-1] * conv_w[1] + up[:, 2:] * conv_w[2]
    y_a = a_b * conv

    lambda_init = 0.8 - 0.6 * math.exp(-0.3 * l)
    q = _rms(d_q.reshape(b, s, DIFF_HEADS, 2, DIFF_QK_DIM), p['diff_q_g'][l])
    k = _rms(d_k.reshape(b, s, DIFF_HEADS, 2, DIFF_QK_DIM), p['diff_k_g'][l])
    v = d_v.reshape(b, s, DIFF_HEADS, DIFF_HEAD_DIM)
    lv = p['diff_lambda'][l].astype(jnp.float32)
    lam = jnp.exp(jnp.sum(lv[0] * lv[1])) - jnp.exp(jnp.sum(lv[2] * lv[3])) + lambda_init
    o = _diff_attention(q[..., 0, :], q[..., 1, :], k[..., 0, :], k[..., 1, :], v, lam,
                        _alibi_slopes(DIFF_HEADS))
    o = _rms(o, p['diff_subln_g'][l]) * (1.0 - lambda_init)
    y_b = o.reshape(b, s, GROUP_WIDTH).astype(dt)

    cq = _rms(m_cq, p['mla_q_a_g'][l])
    qm = jnp.einsum('bsr,re->bse', cq, p['mla_w_uq'][l]).reshape(b, s, MLA_HEADS, MLA_NOPE + MLA_ROPE)
    ckv = _rms(m_ckv, p['mla_kv_a_g'][l])
    kv = jnp.einsum('bsr,re->bse', ckv, p['mla_w_ukv'][l]).reshape(b, s, MLA_HEADS, MLA_NOPE + MLA_V)
    k_nope, vm = jnp.split(kv, [MLA_NOPE], axis=-1)
    k_pe = jnp.broadcast_to(m_kpe[:, :, None, :], (b, s, MLA_HEADS, MLA_ROPE))
    km = jnp.concatenate([k_nope, k_pe], axis=-1)
    qm = _rms(qm, p['mla_q_g'][l])
    km = _rms(km, p['mla_k_g'][l])
    qm = jnp.concatenate([qm[..., :MLA_NOPE], _rope(qm[..., MLA_NOPE:])], axis=-1)
    km = jnp.concatenate([km[..., :MLA_NOPE], _rope(km[..., MLA_NOPE:])], axis=-1)
    y_c = _mla_attention(qm, km, vm).reshape(b, s, GROUP_WIDTH).astype(dt)

    fu = f_u.reshape(b, s, FOURIER_GROUPS, FOURIER_CH // FOURIER_GROUPS).astype(jnp.float32)
    y_d = jnp.fft.fftn(fu, axes=(1, 3), norm='ortho').real.reshape(b, s, FOURIER_CH).astype(dt)

    y = jnp.concatenate([y_a, y_b, y_c, y_d], axis=-1)
    return jnp.einsum('bse,ed->bsd', y, p['w_out'][l])


def _layer(x, c, p, l):
    b = x.shape[0]
    m = (jnp.einsum('bd,de->be', jax.nn.silu(c), p['w_mod'][l]) + p['b_mod'][l]).reshape(b, N_MOD, D_MODEL)
    ng = p['norm_g'][l]

    def modulate(t, i):
        return _rms(t, ng[i]) * (1.0 + m[:, 3 * i + 1, None, :]) + m[:, 3 * i, None, :]

    x = x + 0.5 * m[:, 2, None, :] * _swiglu(modulate(x, 0), p['ffn1_w_gu'][l], p['ffn1_w_down'][l])
    x = x + m[:, 5, None, :] * _token_mixer(modulate(x, 1), p, l)
    x = x + 0.5 * m[:, 8, None, :] * _swiglu(modulate(x, 2), p['ffn2_w_gu'][l], p['ffn2_w_down'][l])
    return x


def _trunk(x, c, p):
    for l in range(DEPTH):
        x = _layer(x, c, p, l)
    return x


def setup_inputs(seed: int = 0) -> dict:
    key = jax.random.key(seed)
    ks = jax.random.split(key, 32)
    f32 = jnp.float32

    def nrm(k, shape, scale):
        return jax.random.normal(k, shape, f32) * scale

    def gain(k, shape):
        return 1.0 + 0.02 * jax.random.normal(k, shape, f32)

    return {
        'x_prompt': nrm(ks[0], (BATCH, SEQ, D_MODEL), 1.0),
        'x_sample': nrm(ks[1], (DEC_BATCH, DEC_SEQ, D_MODEL), 1.0),
        'c_prompt': nrm(ks[2], (BATCH, D_MODEL), 1.0),
        'c_sample': nrm(ks[3], (DEC_BATCH, D_MODEL), 1.0),
        'w_mod': nrm(ks[4], (DEPTH, D_MODEL, N_MOD * D_MODEL), 0.5 * D_MODEL ** -0.5),
        'b_mod': nrm(ks[5], (DEPTH, N_MOD * D_MODEL), 0.02),
        'norm_g': gain(ks[6], (DEPTH, 3, D_MODEL)),
        'ffn1_w_gu': nrm(ks[7], (DEPTH, D_MODEL, 2 * D_FF), D_MODEL ** -0.5),
        'ffn1_w_down': nrm(ks[8], (DEPTH, D_FF, D_MODEL), D_FF ** -0.5),
        'w_in': nrm(ks[9], (DEPTH, D_MODEL, IN_COLS), D_MODEL ** -0.5),
        'conv_w': nrm(ks[10], (DEPTH, CONV_WIDTH, CONV_CH), CONV_WIDTH ** -0.5),
        'diff_lambda': nrm(ks[11], (DEPTH, 4, DIFF_QK_DIM), 0.1),
        'diff_q_g': gain(ks[12], (DEPTH, DIFF_QK_DIM)),
        'diff_k_g': gain(ks[13], (DEPTH, DIFF_QK_DIM)),
        'diff_subln_g': gain(ks[14], (DEPTH, DIFF_HEAD_DIM)),
        'mla_q_a_g': gain(ks[15], (DEPTH, MLA_Q_RANK)),
        'mla_w_uq': nrm(ks[16], (DEPTH, MLA_Q_RANK, MLA_HEADS * (MLA_NOPE + MLA_ROPE)), MLA_Q_RANK ** -0.5),
        'mla_kv_a_g': gain(ks[17], (DEPTH, MLA_KV_RANK)),
        'mla_w_ukv': nrm(ks[18], (DEPTH, MLA_KV_RANK, MLA_HEADS * (MLA_NOPE + MLA_V)), MLA_KV_RANK ** -0.5),
        'mla_q_g': gain(ks[19], (DEPTH, MLA_NOPE + MLA_ROPE)),
        'mla_k_g': gain(ks[20], (DEPTH, MLA_NOPE + MLA_ROPE)),
        'w_out': nrm(ks[21], (DEPTH, MIX_WIDTH, D_MODEL), MIX_WIDTH ** -0.5),
        'ffn2_w_gu': nrm(ks[22], (DEPTH, D_MODEL, 2 * D_FF), D_MODEL ** -0.5),
        'ffn2_w_down': nrm(ks[23], (DEPTH, D_FF, D_MODEL), D_FF ** -0.5),
    }


def reference(x_prompt, x_sample, c_prompt, c_sample, w_mod, b_mod, norm_g, ffn1_w_gu, ffn1_w_down,
              w_in, conv_w, diff_lambda, diff_q_g, diff_k_g, diff_subln_g, mla_q_a_g, mla_w_uq,
              mla_kv_a_g, mla_w_ukv, mla_q_g, mla_k_g, w_out, ffn2_w_gu, ffn2_w_down):
    p = dict(w_mod=w_mod, b_mod=b_mod, norm_g=norm_g, ffn1_w_gu=ffn1_w_gu, ffn1_w_down=ffn1_w_down,
             w_in=w_in, conv_w=conv_w, diff_lambda=diff_lambda, diff_q_g=diff_q_g, diff_k_g=diff_k_g,
             diff_subln_g=diff_subln_g, mla_q_a_g=mla_q_a_g, mla_w_uq=mla_w_uq, mla_kv_a_g=mla_kv_a_g,
             mla_w_ukv=mla_w_ukv, mla_q_g=mla_q_g, mla_k_g=mla_k_g, w_out=w_out,
             ffn2_w_gu=ffn2_w_gu, ffn2_w_down=ffn2_w_down)
    y_prompt = _trunk(x_prompt, c_prompt, p)
    y_sample = _trunk(x_sample, c_sample, p)
    return (y_prompt, y_sample)
```

```python
import contextlib
import math
import numpy as np
import ml_dtypes
import concourse.bass as bass
import concourse.mybir as mybir
from concourse.bass_utils import run_bass_kernel_spmd

F32 = mybir.dt.float32
BF16 = mybir.dt.bfloat16
ALU = mybir.AluOpType
AF = mybir.ActivationFunctionType
AX = mybir.AxisListType

NCORES = 8
D = 1024
KC = 8
FF = 2816
FJ = 22
INC = 2208
DEPTH = 2
EPS = 1e-6
SP_LEN = 16384
SS_LEN = 4096
TTF = 1024
SUB = 512
PROWS = 1920
ENGS = ("pe", "act", "dve", "pool", "sp")
C_AB, C_AC, C_AX, C_DQ, C_DK, C_DV, C_CQ, C_CKV, C_KPE, C_FU = 0, 256, 512, 768, 1024, 1280, 1536, 1792, 1920, 1952
SLOPES = [2.0 ** (-8.0 * (i + 1) / 4) for i in range(4)]


class Op:
    __slots__ = ("eng", "fn", "waits", "sig", "tok", "inc")

    def __init__(self, eng, fn):
        self.eng, self.fn, self.waits, self.sig, self.tok, self.inc = eng, fn, [], None, None, 1


class Sched:
    def __init__(self, n_dma_slots=12):
        self.ops = {e: [] for e in ENGS}
        self.count = {}
        self.lastw = {}
        self.readers = {}
        self.waited = {e: {} for e in ENGS}
        self.n_dma_slots = n_dma_slots
        self.dma_rr = {"sp": 0, "pool": 0, "act": 0}
        self.pbar = {e: None for e in ENGS}

    def _need(self, op, tok):
        if tok is None:
            return
        key, val = tok
        if key == "pe" and op.eng == "pe":
            return
        w = self.waited[op.eng]
        if w.get(key, 0) >= val:
            return
        w[key] = val
        op.waits.append((key, val))

    def barrier(self):
        snap = dict(self.count)
        for e in ENGS:
            self.pbar[e] = snap

    def add(self, eng, fn, reads=(), writes=(), sig=True, dma=False, inc=16):
        op = Op(eng, fn)
        if self.pbar[eng] is not None:
            for k, v in self.pbar[eng].items():
                self._need(op, (k, v))
            self.pbar[eng] = None
        for r in reads:
            self._need(op, self.lastw.get(r))
        for r in writes:
            self._need(op, self.lastw.get(r))
            for t in self.readers.get(r, ()):
                self._need(op, t)
        if dma:
            slot = self.dma_rr[eng]
            self.dma_rr[eng] = (slot + 1) % self.n_dma_slots
            key = ("dma", eng, slot)
            prev = self.count.get(key, 0)
            if prev:
                self._need(op, (key, prev))
            self.count[key] = prev + inc
            op.tok = (key, prev + inc)
            op.sig = key
            op.inc = inc
        else:
            key = eng
            if sig:
                self.count[key] = self.count.get(key, 0) + 1
                op.tok = (key, self.count[key])
                op.sig = key
            else:
                op.tok = (key, self.count.get(key, 0) + 1)
        for r in reads:
            self.readers.setdefault(r, []).append(op.tok)
        for r in writes:
            self.lastw[r] = op.tok
            self.readers[r] = []
        self.ops[eng].append(op)
        return op

    def emit(self, nc, final_wait_eng="sp"):
        keys = sorted(self.count.keys(), key=str)
        with contextlib.ExitStack() as st:
            sems = {}
            for i, k in enumerate(keys):
                sems[k] = st.enter_context(nc.semaphore("s%d" % i))
            block = st.enter_context(nc.Block())
            engmap = {"pe": "tensor", "act": "scalar", "dve": "vector", "pool": "gpsimd", "sp": "sync"}

            def make(e):
                def body(eng):
                    for op in self.ops[e]:
                        for (k, v) in op.waits[1:]:
                            eng.wait_ge(sems[k], v)
                        ins = op.fn(eng)
                        if op.waits:
                            k, v = op.waits[0]
                            ins._wait_ge(sems[k], v)
                        if op.sig is not None:
                            ins.then_inc(sems[op.sig], op.inc)
                    if e == final_wait_eng:
                        for k in keys:
                            eng.wait_ge(sems[k], self.count[k])
                return body

            for e in ENGS:
                getattr(block, engmap[e])(make(e))


class Tile:
    def __init__(self, h, name):
        self.h, self.name = h, name

    def __getitem__(self, idx):
        return self.h[idx]


class Builder:
    def __init__(self, sp_len, ss_len):
        self.SP, self.SS = sp_len, ss_len
        self.TP = sp_len // NCORES
        self.NT = self.TP + 2 * ss_len
        self.N1P = sp_len // 128
        self.N1S = ss_len // 128
        self.NK0P = self.TP // self.N1P
        self.seqs = [dict(off=0, lq=self.TP, S=sp_len, n1=self.N1P, nk0=self.NK0P, prompt=True),
                     dict(off=self.TP, lq=ss_len, S=ss_len, n1=self.N1S, nk0=128, prompt=False),
                     dict(off=self.TP + ss_len, lq=ss_len, S=ss_len, n1=self.N1S, nk0=128, prompt=False)]
        col = 0
        for s in self.seqs:
            s["acol"] = col
            col += (s["lq"] // SUB) * (s["S"] // 128)
        self.NACOL = col
        self.nc = bass.Bass("TRN2", target_bir_lowering=False)
        self.S = Sched()
        self.uid = 0

    def din(self, name, shape, dt=F32):
        return self.nc.dram_tensor(name, list(shape), dt, kind="ExternalInput")

    def dscr(self, name, shape, dt=BF16):
        return self.nc.dram_tensor(name, list(shape), dt, kind="Internal")

    def sb(self, st, name, shape, dt):
        self.uid += 1
        return Tile(st.enter_context(self.nc.sbuf_tensor("%s_%d" % (name, self.uid), list(shape), dt)), name)

    def dma(self, out, in_, reads, writes, eng="sp", slow=False):
        if slow:
            fn = lambda e: e.dma_start(out=out, in_=in_, allow_slow_non_contiguous=True)
        else:
            fn = lambda e: e.dma_start(out=out, in_=in_)
        return self.S.add(eng, fn, reads, writes, dma=True)

    def mm(self, out, lhsT, rhs, start, stop, reads, writes, tp=None):
        if tp is not None:
            return self.S.add("pe", lambda e: e.matmul(out, lhsT=lhsT, rhs=rhs, start=start, stop=stop, tile_position=tp),
                              reads, writes, sig=stop)
        return self.S.add("pe", lambda e: e.matmul(out, lhsT=lhsT, rhs=rhs, start=start, stop=stop),
                          reads, writes, sig=stop)

    def act(self, out, in_, func, reads, writes, bias=None, scale=None):
        kw = {}
        if bias is not None:
            kw["bias"] = bias
        if scale is not None:
            kw["scale"] = scale
        return self.S.add("act", lambda e: e.activation(out=out, in_=in_, func=func, **kw), reads, writes)

    def tt(self, out, in0, in1, op, reads, writes, eng="dve"):
        return self.S.add(eng, lambda e: e.tensor_tensor(out=out, in0=in0, in1=in1, op=op), reads, writes)

    def ts(self, out, in0, s1, s2, op0, op1, reads, writes, eng="dve"):
        if op1 is None:
            return self.S.add(eng, lambda e: e.tensor_single_scalar(out=out, in_=in0, scalar=s1, op=op0), reads, writes)
        return self.S.add(eng, lambda e: e.tensor_scalar(out=out, in0=in0, scalar1=s1, scalar2=s2, op0=op0, op1=op1),
                          reads, writes)

    def stt(self, out, in0, scalar, in1, op0, op1, reads, writes, eng="dve"):
        return self.S.add(eng, lambda e: e.scalar_tensor_tensor(out=out, in0=in0, scalar=scalar, in1=in1, op0=op0, op1=op1),
                          reads, writes)

    def cp(self, out, in_, reads, writes, eng="dve"):
        return self.S.add(eng, lambda e: e.tensor_copy(out=out, in_=in_), reads, writes)

    def memset(self, ap, val, writes, eng="pool"):
        return self.S.add(eng, lambda e: e.memset(ap, val), (), writes)

    def rsqrt_from_sum(self, out, psum_in, n, reads, writes):
        self.ts(out, psum_in, 1.0 / n, EPS, ALU.mult, ALU.add, reads, writes)
        self.act(out, out, AF.Sqrt, writes, writes)
        self.S.add("dve", lambda e: e.reciprocal(out=out, in_=out), writes, writes)

    def pay_rows(self, pay, r0, nrows, c0, ncols):
        t, o, L = pay
        return bass.AP(t, o + r0 * L + c0, [[L, nrows], [1, ncols]])

    def pay_tok(self, pay, sec, t0, ntok, c0, ncols):
        t, o, L = pay
        return bass.AP(t, o + (896 + 256 * sec) * L + t0 * 256 + c0, [[256, ntok], [1, ncols]])

    def key_pay(self, si, k0):
        s = self.seqs[si]
        if s["prompt"]:
            r = k0 // self.TP
            return (self.gout, r * PROWS * self.TP, self.TP), k0 % self.TP
        return (self.lpay[si], 0, self.SS), k0

    def loc_pay(self, si):
        if self.seqs[si]["prompt"]:
            return (self.gin, 0, self.TP)
        return (self.lpay[si], 0, self.SS)

    def build(self):
        nc, S = self.nc, self.S
        NT, TP, SS = self.NT, self.TP, self.SS
        self.xT = self.din("xT", [D, NT])
        self.cT = self.din("cT", [128, KC * 3])
        self.w_mod = self.din("w_mod", [DEPTH, D, 9 * D])
        self.b_mod_r = self.din("b_mod_r", [DEPTH, 128, 72])
        self.norm_g_r = self.din("norm_g_r", [DEPTH, 128, 24])
        self.wgu_r = self.din("wgu_r", [DEPTH * 2 * FJ, 128, KC * 256])
        self.wdn_r = self.din("wdn_r", [DEPTH * 2 * KC, 128, FJ * 128])
        self.w_in_r = self.din("w_in_r", [DEPTH, 128, KC * INC])
        self.w_out_r = self.din("w_out_r", [DEPTH, 128, KC * D])
        self.wuq_r = self.din("wuq_r", [DEPTH, 128, 2 * 2 * 384])
        self.wukn_r = self.din("wukn_r", [DEPTH, 128, 4 * 96])
        self.wukv_r = self.din("wukv_r", [DEPTH, 128, 256])
        self.wkpe_r = self.din("wkpe_r", [DEPTH, 128, KC * 2 * 96])
        self.vecs_r = self.din("vecs_r", [DEPTH, 128, 16])
        self.lam_r = self.din("lam_r", [DEPTH, 128, 128])
        self.ropeC = self.din("ropeC", [32, NT])
        self.ropeS = self.din("ropeS", [32, NT])
        self.alibi_c = self.din("alibi_c", [128, self.NACOL])
        self.dtab = self.din("dtab", [128, SUB])
        self.cmats = self.din("cmats", [128, 5 * 128], BF16)
        self.f1p = self.din("f1p", [self.N1P, 4 * self.N1P], BF16)
        self.f1s = self.din("f1s", [self.N1S, 4 * self.N1S], BF16)
        self.twp = self.din("twp", [128, 2 * self.N1P])
        self.tws = self.din("tws", [128, 2 * self.N1S])
        self.f3p = self.din("f3p", [128, 2 * self.NK0P], BF16)
        self.f3s = self.din("f3s", [128, 2 * 128], BF16)
        self.emask = self.din("emask", [128, 16])
        self.yT = nc.dram_tensor("yT", [D, NT], F32, kind="ExternalOutput")
        self.wgu_s = self.dscr("wgu_s", [DEPTH * 2 * FJ, 128, KC * 256])
        self.wdn_s = self.dscr("wdn_s", [DEPTH * 2 * KC, 128, FJ * 128])
        self.x1s = self.dscr("x1s", [D, NT], F32)
        self.ymix = self.dscr("ymix", [D, NT])
        self.ab_s = self.dscr("ab_s", [256, NT])
        self.qT_s = self.dscr("qT_s", [256, NT])
        self.qm_s = self.dscr("qm_s", [384, NT])
        self.gin = self.dscr("gin", [PROWS, TP])
        self.gout = self.dscr("gout", [NCORES * PROWS, TP])
        self.lpay = {1: self.dscr("lpay1", [PROWS, SS]), 2: self.dscr("lpay2", [PROWS, SS])}

        with contextlib.ExitStack() as gst:
            self.ps = [Tile(gst.enter_context(nc.psum_tensor("ps%d" % i, [128, 512], F32)), "ps%d" % i) for i in range(8)]
            self.cm = self.sb(gst, "cm", [128, 5 * 128], BF16)
            self.modA = self.sb(gst, "modA", [128, DEPTH * 3 * 3 * KC], F32)
            self.modB = self.sb(gst, "modB", [128, DEPTH * 3 * 3 * KC], F32)
            self.modG = self.sb(gst, "modG", [128, DEPTH * 3 * 3 * KC], F32)
            self.vecs = self.sb(gst, "vecs", [128, DEPTH * 16], F32)
            self.nlam = self.sb(gst, "nlam", [128, DEPTH], F32)
            self.dma(self.cm[:], self.cmats.ap(), (), ["cm"])
            for l in range(DEPTH):
                self.dma(self.vecs[:, l * 16:(l + 1) * 16], self.vecs_r.ap()[l], (), ["vecs"])
            self.ones = self.cm[:, 0:128]
            self.blk32 = self.cm[:, 128:256]
            self.blk64 = self.cm[:, 256:384]
            self.c64 = self.cm[:, 384:512]
            self.s64 = self.cm[:, 512:640]
            self.setup_weights()
            self.setup_mod()
            import os
            stop = int(os.environ.get("MK_STOP", "99"))
            phases = [lambda: self.dense_phase(0), lambda: self.mixer_phase(0), lambda: self.dense_phase(1),
                      lambda: self.mixer_phase(1), lambda: self.dense_phase(2)]
            for ph in phases[:stop]:
                ph()
            S.emit(nc)
        return nc

    def mvec(self, t, l, i, j, k):
        c = ((l * 3 + i) * 3 + j) * KC + k
        return t[:, c:c + 1]

    def vcol(self, l, c, p0=0, p1=128):
        return self.vecs[p0:p1, l * 16 + c:l * 16 + c + 1]

    def setup_weights(self):
        S = self.S
        S.barrier()
        with contextlib.ExitStack() as st:
            bufs = [self.sb(st, "wcv%d" % i, [128, FJ * 128], BF16) for i in range(3)]
            n = 0
            for t in range(DEPTH * 2 * FJ):
                b = bufs[n % 3]; n += 1
                self.dma(b[:, 0:KC * 256], self.wgu_r.ap()[t], (), [b.name], eng="pool")
                self.dma(self.wgu_s.ap()[t], b[:, 0:KC * 256], [b.name], [("wgu_s", t)])
            for t in range(DEPTH * 2 * KC):
                b = bufs[n % 3]; n += 1
                self.dma(b[:, 0:FJ * 128], self.wdn_r.ap()[t], (), [b.name], eng="pool")
                self.dma(self.wdn_s.ap()[t], b[:, 0:FJ * 128], [b.name], [("wdn_s", t)])

    def setup_mod(self):
        S = self.S
        S.barrier()
        with contextlib.ExitStack() as st:
            ct = self.sb(st, "ct", [128, KC * 3], F32)
            sc = self.sb(st, "sc", [128, KC * 3], F32)
            bm = self.sb(st, "bm", [128, 72], F32)
            ng = self.sb(st, "ng", [128, 24], F32)
            mv = self.sb(st, "mv", [128, 72 * 3], F32)
            lamt = self.sb(st, "lamt", [128, 128], F32)
            lp = self.sb(st, "lp", [128, 64], F32)
            ls = self.sb(st, "ls", [128, 4], F32)
            wm = [self.sb(st, "wm%d" % i, [128, KC * 128], F32) for i in range(3)]
            self.dma(ct[:], self.cT.ap(), (), ["ct"])
            self.act(sc[:], ct[:], AF.Silu, ["ct"], ["sc"])
            psM = self.ps[0]
            for l in range(DEPTH):
                self.dma(bm[:], self.b_mod_r.ap()[l], (), ["bm"])
                self.dma(ng[:], self.norm_g_r.ap()[l], (), ["ng"])
                for ec in range(72):
                    w = wm[ec % 3]
                    src = bass.AP(self.w_mod, l * D * 9 * D + ec * 128, [[9 * D, 128], [128 * 9 * D, KC], [1, 128]])
                    self.dma(w[:].rearrange("p (k c) -> p k c", k=KC), src, (), [w.name])
                    for k in range(KC):
                        self.mm(psM[:, ec * 4:ec * 4 + 3], w[:, k * 128:(k + 1) * 128], sc[:, k * 3:k * 3 + 3],
                                k == 0, k == KC - 1, [w.name, "sc"], ["psM"])
                psv = psM[:, 0:288].rearrange("p (e f) -> p e f", f=4)
                mvv = mv[:].rearrange("p (e j) -> p e j", j=3)
                for j in range(3):
                    self.tt(mvv[:, :, j], psv[:, :, j], bm[:], ALU.add, ["psM", "bm"], ["mv"])
                for i in range(3):
                    for j in range(3):
                        c0 = ((l * 3 + i) * 3 + j) * KC
                        sh = mvv[:, (3 * i) * KC:(3 * i + 1) * KC, j]
                        scl = mvv[:, (3 * i + 1) * KC:(3 * i + 2) * KC, j]
                        gt = mvv[:, (3 * i + 2) * KC:(3 * i + 3) * KC, j]
                        self.stt(self.modA[:, c0:c0 + KC], scl, 1.0, ng[:, i * KC:(i + 1) * KC], ALU.add, ALU.mult,
                                 ["mv", "ng"], ["modA"])
                        self.cp(self.modB[:, c0:c0 + KC], sh, ["mv"], ["modB"])
                        self.ts(self.modG[:, c0:c0 + KC], gt, (1.0 if i == 1 else 0.5), None, ALU.mult, None, ["mv"], ["modG"])
                lam_init = 0.8 - 0.6 * math.exp(-0.3 * l)
                self.dma(lamt[:], self.lam_r.ap()[l], (), ["lamt"])
                self.tt(lp[:, 0:32], lamt[:, 0:32], lamt[:, 32:64], ALU.mult, ["lamt"], ["lp"])
                self.tt(lp[:, 32:64], lamt[:, 64:96], lamt[:, 96:128], ALU.mult, ["lamt"], ["lp"])
                self.S.add("dve", lambda e, o=ls[:, 0:1], i_=lp[:, 0:32]: e.reduce_sum(out=o, in_=i_, axis=AX.X), ["lp"], ["ls"])
                self.S.add("dve", lambda e, o=ls[:, 1:2], i_=lp[:, 32:64]: e.reduce_sum(out=o, in_=i_, axis=AX.X), ["lp"], ["ls"])
                self.act(ls[:, 2:4], ls[:, 0:2], AF.Exp, ["ls"], ["ls"])
                self.stt(self.nlam[:, l:l + 1], ls[:, 2:3], -1.0, ls[:, 3:4], ALU.mult, ALU.add, ["ls"], ["nlam"])
                self.ts(self.nlam[:, l:l + 1], self.nlam[:, l:l + 1], -lam_init, None, ALU.add, None, ["nlam"], ["nlam"])

    def norm_mod(self, T, l, i, j, sub):
        x, h, sq, rstd, tmp = T["x"], T["h"], T["sq"], T["rstd"], T["tmp"]
        psS = self.ps[6]
        sl = slice(sub * SUB, (sub + 1) * SUB)
        for k in range(KC):
            self.act(sq[:, k, :], x[:, k, sl], AF.Square, ["x"], [("sq", k)])
        for k in range(KC):
            self.mm(psS[:], self.ones, sq[:, k, :], k == 0, k == KC - 1, [("sq", k), "cm"], ["ps6"])
        self.rsqrt_from_sum(rstd[:], psS[:], D, ["ps6"], ["rstd"])
        for k in range(KC):
            t = tmp[k % 2]
            self.tt(t[:], x[:, k, sl], rstd[:], ALU.mult, ["x", "rstd"], [t.name])
            self.act(h[:, k, sl], t[:], AF.Identity, [t.name, "modA", "modB"], [("h", k, sub)],
                     bias=self.mvec(self.modB, l, i, j, k), scale=self.mvec(self.modA, l, i, j, k))

    def ffn(self, T, l, f, j):
        x, h, a = T["x"], T["h"], T["a"]
        nsub = TTF // SUB
        for jf in range(FJ):
            w = T["wgu"][jf % len(T["wgu"])]
            t = (l * 2 + f) * FJ + jf
            self.dma(w[:], self.wgu_s.ap()[t], [("wgu_s", t)], [w.name])
            wv = w[:].rearrange("p (k c) -> p k c", k=KC)
            for sub in range(nsub):
                sl = slice(sub * SUB, (sub + 1) * SUB)
                pg, pu = self.ps[2 * sub], self.ps[2 * sub + 1]
                for k in range(KC):
                    self.mm(pg[:], wv[:, k, 0:128], h[:, k, sl], k == 0, k == KC - 1, [w.name, ("h", k, sub)], [pg.name])
                for k in range(KC):
                    self.mm(pu[:], wv[:, k, 128:256], h[:, k, sl], k == 0, k == KC - 1, [w.name, ("h", k, sub)], [pu.name])
                sg = T["sg"][sub % 2]
                self.act(sg[:], pg[:], AF.Silu, [pg.name], [sg.name])
                self.tt(a[:, jf, sl], sg[:], pu[:], ALU.mult, [sg.name, pu.name], [("a", jf, sub)])
        for i in range(KC):
            w = T["wdn"][i % 2]
            t = (l * 2 + f) * KC + i
            self.dma(w[:], self.wdn_s.ap()[t], [("wdn_s", t)], [w.name])
            wv = w[:].rearrange("p (j c) -> p j c", j=FJ)
            for sub in range(nsub):
                sl = slice(sub * SUB, (sub + 1) * SUB)
                pd = self.ps[4 + sub]
                for jf in range(FJ):
                    self.mm(pd[:], wv[:, jf, :], a[:, jf, sl], jf == 0, jf == FJ - 1, [w.name, ("a", jf, sub)], [pd.name])
                self.stt(x[:, i, sl], pd[:], self.mvec(self.modG, l, 2 * f, j, i), x[:, i, sl], ALU.mult, ALU.add,
                         [pd.name, "x", "modG"], ["x"])

    def dense_phase(self, ph):
        S = self.S
        S.barrier()
        NT = self.NT
        with contextlib.ExitStack() as st:
            T = {}
            T["x"] = self.sb(st, "x", [128, KC, TTF], F32)
            T["h"] = self.sb(st, "h", [128, KC, TTF], BF16)
            T["a"] = self.sb(st, "a", [128, FJ, TTF], BF16)
            T["sq"] = self.sb(st, "sq", [128, KC, SUB], BF16)
            T["rstd"] = self.sb(st, "rstd", [128, SUB], F32)
            T["tmp"] = [self.sb(st, "tmp%d" % i, [128, SUB], F32) for i in range(2)]
            T["sg"] = [self.sb(st, "sg%d" % i, [128, SUB], F32) for i in range(2)]
            T["wgu"] = [self.sb(st, "wgu%d" % i, [128, KC * 256], BF16) for i in range(2 if ph == 1 else 3)]
            T["wdn"] = [self.sb(st, "wdn%d" % i, [128, FJ * 128], BF16) for i in range(2)]
            if ph < 2:
                T["win"] = self.sb(st, "win", [128, KC, INC], BF16)
                T["wuq"] = self.sb(st, "wuq", [128, 2, 2, 384], BF16)
                T["wukn"] = self.sb(st, "wukn", [128, 4, 96], BF16)
                T["wukv"] = self.sb(st, "wukv", [128, 256], BF16)
                T["wkpe"] = self.sb(st, "wkpe", [128, KC, 2, 96], BF16)
                T["stg"] = [self.sb(st, "stg%d" % i, [128, SUB], BF16) for i in range(3)]
                T["nst"] = 0
                T["t32"] = T["tmp"] + [self.sb(st, "t32_2", [128, SUB], F32)]
                T["cq"] = self.sb(st, "cq", [128, 2, SUB], BF16)
                T["ckv"] = self.sb(st, "ckv", [128, SUB], BF16)
                T["fu"] = self.sb(st, "fu", [128, 2, SUB], BF16)
                T["tc"] = self.sb(st, "tc", [96, SUB], F32)
                T["tsn"] = self.sb(st, "tsn", [96, SUB], F32)
                l = ph
                self.dma(T["win"][:].rearrange("p k c -> p (k c)"), self.w_in_r.ap()[l], (), ["win"], eng="pool")
                self.dma(T["wuq"][:].rearrange("p a b c -> p (a b c)"), self.wuq_r.ap()[l], (), ["wuq"], eng="pool")
                self.dma(T["wukn"][:].rearrange("p a c -> p (a c)"), self.wukn_r.ap()[l], (), ["wukn"], eng="pool")
                self.dma(T["wukv"][:], self.wukv_r.ap()[l], (), ["wukv"], eng="pool")
                self.dma(T["wkpe"][:].rearrange("p a b c -> p (a b c)"), self.wkpe_r.ap()[l], (), ["wkpe"], eng="pool")
                self.memset(T["tc"][0:64, :], 1.0, ["tc"])
                self.memset(T["tsn"][0:64, :], 0.0, ["tsn"])
            if ph > 0:
                T["wout"] = self.sb(st, "wout", [128, KC, D], BF16)
                self.dma(T["wout"][:].rearrange("p k c -> p (k c)"), self.w_out_r.ap()[ph - 1], (), ["wout"], eng="pool")
            x, h = T["x"], T["h"]
            for si, sq_ in enumerate(self.seqs):
                for t0 in range(sq_["off"], sq_["off"] + sq_["lq"], TTF):
                    src = self.xT if ph == 0 else self.x1s
                    srcap = bass.AP(src, t0, [[NT, 128], [128 * NT, KC], [1, TTF]])
                    self.dma(x[:], srcap, ["x1s"] if ph else (), ["x"])
                    j = si
                    if ph > 0:
                        lp = ph - 1
                        yap = bass.AP(self.ymix, t0, [[NT, 128], [128 * NT, KC], [1, TTF]])
                        self.dma(h[:], yap, ["ymix"], [("h", k, s_) for k in range(KC) for s_ in range(2)])
                        for i in range(KC):
                            for sub in range(TTF // SUB):
                                sl = slice(sub * SUB, (sub + 1) * SUB)
                                pd = self.ps[4 + sub]
                                for k in range(KC):
                                    self.mm(pd[:], T["wout"][:, k, i * 128:(i + 1) * 128], h[:, k, sl], k == 0, k == KC - 1,
                                            ["wout", ("h", k, sub)], [pd.name])
                                self.stt(x[:, i, sl], pd[:], self.mvec(self.modG, lp, 1, j, i), x[:, i, sl], ALU.mult, ALU.add,
                                         [pd.name, "x", "modG"], ["x"])
                        for sub in range(TTF // SUB):
                            self.norm_mod(T, lp, 2, j, sub)
                        self.ffn(T, lp, 1, j)
                    if ph < 2:
                        l = ph
                        for sub in range(TTF // SUB):
                            self.norm_mod(T, l, 0, j, sub)
                        self.ffn(T, l, 0, j)
                        dst = bass.AP(self.x1s, t0, [[NT, 128], [128 * NT, KC], [1, TTF]])
                        self.dma(dst, x[:], ["x"], ["x1s"], eng="pool")
                        for sub in range(TTF // SUB):
                            self.norm_mod(T, l, 1, j, sub)
                            self.proj(T, l, si, t0, sub)
                    else:
                        dst = bass.AP(self.yT, t0, [[NT, 128], [128 * NT, KC], [1, TTF]])
                        self.dma(dst, x[:], ["x"], ["yT"], eng="pool")

    def stage(self, T):
        s = T["stg"][T["nst"] % 3]
        T["nst"] += 1
        return s

    def proj(self, T, l, si, t0, sub):
        h, win = T["h"], T["win"]
        sl = slice(sub * SUB, (sub + 1) * SUB)
        tg = t0 + sub * SUB
        seq = self.seqs[si]
        tl = tg - seq["off"]
        pay = self.loc_pay(si)
        hr = [("h", k, sub) for k in range(KC)]
        P = self.ps

        def fm(ps, c0, m):
            for k in range(KC):
                self.mm(ps[0:m, :], win[:, k, c0:c0 + m], h[:, k, sl], k == 0, k == KC - 1, ["win"] + hr, [ps.name])

        self.dma(T["tc"][64:96, :], bass.AP(self.ropeC, tg, [[self.NT, 32], [1, SUB]]), (), ["tc"])
        self.dma(T["tsn"][64:96, :], bass.AP(self.ropeS, tg, [[self.NT, 32], [1, SUB]]), (), ["tsn"])
        for c in range(2):
            fm(P[0], C_AB + c * 128, 128)
            sg = self.stage(T)
            self.act(sg[:], P[0][:], AF.Copy, ["ps0"], [sg.name])
            self.dma(bass.AP(self.ab_s, c * 128 * self.NT + tg, [[self.NT, 128], [1, SUB]]), sg[:], [sg.name], ["ab_s"], eng="pool")
            fm(P[1], C_AC + c * 128, 128)
            fm(P[2], C_AX + c * 128, 128)
            t = T["t32"][0]
            self.act(t[:], P[1][:], AF.Copy, ["ps1"], [t.name])
            sg = self.stage(T)
            self.tt(sg[:], t[:], P[2][:], ALU.mult, [t.name, "ps2"], [sg.name])
            self.dma(self.pay_rows(pay, 640 + c * 128, 128, tl, SUB), sg[:], [sg.name], ["pay"], eng="pool")
        for which, c0, gcol in (("q", C_DQ, 0), ("k", C_DK, 1)):
            for c in range(2):
                fm(P[0], c0 + c * 128, 128)
                sq = T["sq"]
                self.act(sq[:, 0, :], P[0][:], AF.Square, ["ps0"], [("sq", 0)])
                self.mm(P[1][:], self.blk32, sq[:, 0, :], True, True, [("sq", 0), "cm"], ["ps1"])
                r = T["t32"][1]
                self.rsqrt_from_sum(r[:], P[1][:], 32, ["ps1"], [r.name])
                t = T["t32"][2]
                self.tt(t[:], P[0][:], r[:], ALU.mult, ["ps0", r.name], [t.name])
                sg = self.stage(T)
                self.act(sg[:], t[:], AF.Copy, [t.name, "vecs"], [sg.name], scale=self.vcol(l, gcol))
                if which == "q":
                    dst = bass.AP(self.qT_s, c * 128 * self.NT + tg, [[self.NT, 128], [1, SUB]])
                    self.dma(dst, sg[:], [sg.name], ["qT_s"], eng="pool")
                else:
                    self.dma(self.pay_rows(pay, c * 128, 128, tl, SUB), sg[:], [sg.name], ["pay"], eng="pool")
        for tc in range(SUB // 128):
            pv = P[2 + tc % 2]
            for k in range(KC):
                self.mm(pv[:, 0:256], h[:, k, sub * SUB + tc * 128: sub * SUB + (tc + 1) * 128], win[:, k, C_DV:C_DV + 256],
                        k == 0, k == KC - 1, ["win"] + hr, [pv.name])
            sg = self.stage(T)
            self.act(sg[:, 0:256], pv[:, 0:256], AF.Copy, [pv.name], [sg.name])
            self.dma(self.pay_tok(pay, 0, tl + tc * 128, 128, 0, 256), sg[:, 0:256], [sg.name], ["pay"], eng="pool")
        fm(P[0], C_CQ, 128)
        fm(P[1], C_CQ + 128, 128)
        sq = T["sq"]
        self.act(sq[:, 0, :], P[0][:], AF.Square, ["ps0"], [("sq", 0)])
        self.act(sq[:, 1, :], P[1][:], AF.Square, ["ps1"], [("sq", 1)])
        self.mm(P[2][:], self.ones, sq[:, 0, :], True, False, [("sq", 0), "cm"], ["ps2"])
        self.mm(P[2][:], self.ones, sq[:, 1, :], False, True, [("sq", 1), "cm"], ["ps2"])
        r = T["t32"][1]
        self.rsqrt_from_sum(r[:], P[2][:], 256, ["ps2"], [r.name])
        for c in range(2):
            t = T["t32"][2]
            self.tt(t[:], P[c][:], r[:], ALU.mult, [P[c].name, r.name], [t.name])
            self.act(T["cq"][:, c, :], t[:], AF.Copy, [t.name, "vecs"], [("cq", c)], scale=self.vcol(l, 2 + c))
        for hh in range(4):
            pt, pw = P[0], P[1]
            for c in range(2):
                self.mm(pt[0:96, :], T["wuq"][:, c, 0, hh * 96:(hh + 1) * 96], T["cq"][:, c, :], c == 0, c == 1,
                        ["wuq", ("cq", 0), ("cq", 1)], ["ps0"])
            for c in range(2):
                self.mm(pw[0:96, :], T["wuq"][:, c, 1, hh * 96:(hh + 1) * 96], T["cq"][:, c, :], c == 0, c == 1,
                        ["wuq", ("cq", 0), ("cq", 1)], ["ps1"])
            dst = bass.AP(self.qm_s, hh * 96 * self.NT + tg, [[self.NT, 96], [1, SUB]])
            self.rope_head(T, l, pt, pw, 6, 7, dst, "qm_s")
        fm(P[0], C_CKV, 128)
        self.act(sq[:, 0, :], P[0][:], AF.Square, ["ps0"], [("sq", 0)])
        self.mm(P[2][:], self.ones, sq[:, 0, :], True, True, [("sq", 0), "cm"], ["ps2"])
        r = T["t32"][1]
        self.rsqrt_from_sum(r[:], P[2][:], 128, ["ps2"], [r.name])
        t = T["t32"][2]
        self.tt(t[:], P[0][:], r[:], ALU.mult, ["ps0", r.name], [t.name])
        self.act(T["ckv"][:], t[:], AF.Copy, [t.name, "vecs"], ["ckv"], scale=self.vcol(l, 4))
        for hh in range(4):
            pt, pw = P[0], P[1]
            self.mm(pt[0:96, :], T["wukn"][:, hh, :], T["ckv"][:], True, False, ["wukn", "ckv"], ["ps0"])
            for k in range(KC):
                self.mm(pt[0:96, :], T["wkpe"][:, k, 0, :], h[:, k, sl], False, k == KC - 1, ["wkpe"] + hr, ["ps0"])
            for k in range(KC):
                self.mm(pw[0:96, :], T["wkpe"][:, k, 1, :], h[:, k, sl], k == 0, k == KC - 1, ["wkpe"] + hr, ["ps1"])
            dst = self.pay_rows(pay, 256 + hh * 96, 96, tl, SUB)
            self.rope_head(T, l, pt, pw, 8, 9, dst, "pay")
        for tc in range(SUB // 128):
            pv = P[2 + tc % 2]
            self.mm(pv[:, 0:256], T["ckv"][:, tc * 128:(tc + 1) * 128], T["wukv"][:], True, True, ["ckv", "wukv"], [pv.name])
            sg = self.stage(T)
            self.act(sg[:, 0:256], pv[:, 0:256], AF.Copy, [pv.name], [sg.name])
            self.dma(self.pay_tok(pay, 1, tl + tc * 128, 128, 0, 256), sg[:, 0:256], [sg.name], ["pay"], eng="pool")
        for c in range(2):
            fm(P[c], C_FU + c * 128, 128)
            self.act(T["fu"][:, c, :], P[c][:], AF.Copy, [P[c].name], [("fu", c)])
        for tc in range(SUB // 128):
            for which, mat in ((2, self.c64), (3, self.s64)):
                pv = P[2 + (which % 2)]
                for c in range(2):
                    self.mm(pv[:, c * 128:(c + 1) * 128], T["fu"][:, c, tc * 128:(tc + 1) * 128], mat, True, True,
                            [("fu", c), "cm"], [pv.name])
                sg = self.stage(T)
                self.act(sg[:, 0:256], pv[:, 0:256], AF.Copy, [pv.name], [sg.name])
                self.dma(self.pay_tok(pay, which, tl + tc * 128, 128, 0, 256), sg[:, 0:256], [sg.name], ["pay"], eng="pool")

    def rope_head(self, T, l, pt, pw, gc, gsc, dst, dname):
        sq = T["sq"]
        P = self.ps
        self.act(sq[0:96, 2, :], pt[0:96, :], AF.Square, [pt.name], [("sq", 2)])
        self.mm(P[2][0:96, :], self.ones[0:96, 0:96], sq[0:96, 2, :], True, True, [("sq", 2), "cm"], ["ps2"])
        r = T["t32"][1]
        self.rsqrt_from_sum(r[0:96, :], P[2][0:96, :], 96, ["ps2"], [r.name])
        a = T["t32"][0]
        b = T["t32"][2]
        self.stt(a[0:96, :], pt[0:96, :], self.vcol(l, gc, 0, 96), T["tc"][:], ALU.mult, ALU.mult, [pt.name, "tc", "vecs"], [a.name])
        self.stt(b[0:96, :], pw[0:96, :], self.vcol(l, gsc, 0, 96), T["tsn"][:], ALU.mult, ALU.mult, [pw.name, "tsn", "vecs"], [b.name])
        self.tt(a[0:96, :], a[0:96, :], b[0:96, :], ALU.add, [a.name, b.name], [a.name], eng="pool")
        sg = self.stage(T)
        self.tt(sg[0:96, :], a[0:96, :], r[0:96, :], ALU.mult, [a.name, r.name], [sg.name])
        self.dma(dst, sg[0:96, :], [sg.name], [dname], eng="pool")

    def mixer_phase(self, l):
        S = self.S
        S.barrier()
        op = self.S.add("pool", lambda e: e.collective_compute("AllGather", ALU.bypass, replica_groups=[list(range(NCORES))],
                                                               ins=[self.gin.ap()], outs=[self.gout.ap()]),
                        ["pay"], ["pay"], dma=True, inc=1)
        S.barrier()
        import os
        mix = os.environ.get("MK_MIX", "fa")
        for si in range(3):
            if "f" in mix:
                self.fourier(l, si)
        S.barrier()
        for si in range(3):
            if "a" not in mix:
                break
            s = self.seqs[si]
            qg = min(1024, s["lq"])
            for g0 in range(0, s["lq"], qg):
                self.attention(l, si, g0, qg)

    def fourier(self, l, si):
        s = self.seqs[si]
        n1, nk0, Sl, lq = s["n1"], s["nk0"], s["S"], s["lq"]
        f1 = self.f1p if s["prompt"] else self.f1s
        tw = self.twp if s["prompt"] else self.tws
        f3 = self.f3p if s["prompt"] else self.f3s
        P = self.ps
        self.S.barrier()
        with contextlib.ExitStack() as st:
            zu = self.sb(st, "zu", [128, 128, 128], BF16)
            zv = self.sb(st, "zv", [128, 128, 128], BF16)
            bre = self.sb(st, "bre", [128, n1, 128], BF16)
            bim = self.sb(st, "bim", [128, n1, 128], BF16)
            r1 = self.sb(st, "r1", [128, 4 * n1], BF16)
            twt = self.sb(st, "twt", [128, 2 * n1], F32)
            r3 = self.sb(st, "r3", [128, 2 * nk0], BF16)
            yd = self.sb(st, "yd", [128, lq], BF16)
            tm = [self.sb(st, "ftm%d" % i, [128, n1], F32) for i in range(4)]
            self.dma(r1[0:n1, :], f1.ap(), (), ["r1"])
            self.dma(twt[:], tw.ap(), (), ["twt"])
            self.dma(r3[:], f3.ap(), (), ["r3"])
            tcs, tss = twt[:, 0:n1], twt[:, n1:2 * n1]
            for cb in range(2):
                for (z, sec) in ((zu, 2), (zv, 3)):
                    if s["prompt"]:
                        ppr = self.TP // 128
                        for r in range(NCORES):
                            pay = (self.gout, r * PROWS * self.TP, self.TP)
                            t_, o_, L_ = pay
                            for p0 in range(0, ppr, 8):
                                pn = min(8, ppr - p0)
                                src = bass.AP(t_, o_ + (896 + 256 * sec) * L_ + p0 * 128 * 256 + cb * 128, [[128 * 256, pn], [256, 128], [1, 128]])
                                self.dma(z[r * ppr + p0:r * ppr + p0 + pn, :, :], src, ["pay"], [z.name])
                    else:
                        t_, o_, L_ = self.loc_pay(si)
                        for p0 in range(0, n1, 8):
                            pn = min(8, n1 - p0)
                            src = bass.AP(t_, o_ + (896 + 256 * sec) * L_ + p0 * 128 * 256 + cb * 128, [[128 * 256, pn], [256, 128], [1, 128]])
                            self.dma(z[p0:p0 + pn, :, :], src, ["pay"], [z.name])
                for c in range(128):
                    pa = P[c % 4]
                    self.mm(pa[:, 0:2 * n1], zu[0:n1, :, c], r1[0:n1, 0:2 * n1], True, False, ["zu", "r1"], [pa.name])
                    self.mm(pa[:, 0:2 * n1], zv[0:n1, :, c], r1[0:n1, 2 * n1:4 * n1], False, True, ["zv", "r1"], [pa.name])
                    ar, an = pa[:, 0:n1], pa[:, n1:2 * n1]
                    t0_, t1_, t2_, t3_ = tm
                    self.tt(t0_[:], ar, tcs, ALU.mult, [pa.name, "twt"], [t0_.name])
                    self.tt(t1_[:], an, tss, ALU.mult, [pa.name, "twt"], [t1_.name])
                    self.tt(t2_[:], an, tcs, ALU.mult, [pa.name, "twt"], [t2_.name])
                    self.tt(t3_[:], ar, tss, ALU.mult, [pa.name, "twt"], [t3_.name])
                    self.tt(bre[:, :, c], t0_[:], t1_[:], ALU.subtract, [t0_.name, t1_.name], ["bre"], eng="pool")
                    self.tt(bim[:, :, c], t2_[:], t3_[:], ALU.add, [t2_.name, t3_.name], ["bim"], eng="pool")
                per = 512 // nk0
                ydv = yd[:].rearrange("p (a b) -> p b a", b=n1)
                for g in range(0, n1, per):
                    py = P[4 + (g // per) % 2]
                    for q in range(per):
                        k1 = g + q
                        self.mm(py[:, q * nk0:(q + 1) * nk0], bre[:, k1, :], r3[:, 0:nk0], True, False, ["bre", "r3"], [py.name])
                        self.mm(py[:, q * nk0:(q + 1) * nk0], bim[:, k1, :], r3[:, nk0:2 * nk0], False, True, ["bim", "r3"], [py.name])
                    self.act(ydv[:, g:g + per, :], py[:].rearrange("p (a b) -> p a b", b=nk0), AF.Copy, [py.name], ["yd"])
                dst = bass.AP(self.ymix, (768 + cb * 128) * self.NT + s["off"], [[self.NT, 128], [1, lq]])
                self.dma(dst, yd[:], ["yd"], ["ymix"], eng="pool")

    def flush_av(self, pend, n):
        for _ in range(n):
            acc, vxb, kc, ptb, first, last = pend.pop(0)
            self.mm(acc[:], vxb[:, kc, :], ptb[:], first, last, [vxb.name, ptb.name], [acc.name])

    def attention(self, l, si, g0, qg):
        s = self.seqs[si]
        Sl = s["S"]
        nqt = qg // SUB
        nkc = Sl // 128
        KB = 512
        P = self.ps
        tq0 = s["off"] + g0
        lam_init = 0.8 - 0.6 * math.exp(-0.3 * l)
        self.S.barrier()
        with contextlib.ExitStack() as st:
            qT = self.sb(st, "qT", [128, 2, qg], BF16)
            NKT = 3
            qm = [self.sb(st, "qm%d" % i, [96, qg], BF16) for i in range(2)]
            kt = [self.sb(st, "kt%d" % i, [128, KB], BF16) for i in range(NKT)]
            vx = [self.sb(st, "vx%d" % i, [128, KB // 128, 128], BF16) for i in range(NKT)]
            dt_ = self.sb(st, "dtab", [128, SUB], F32)
            ac = self.sb(st, "alc", [128, self.NACOL], F32)
            tb = [self.sb(st, "tb%d" % i, [128, SUB], F32) for i in range(3)]
            ab_ = [self.sb(st, "abb%d" % i, [128, SUB], F32) for i in range(5)]
            ub = [self.sb(st, "ub%d" % i, [128, SUB], F32) for i in range(4)]
            pt = [self.sb(st, "pt%d" % i, [128, SUB], BF16) for i in range(5)]
            rs = self.sb(st, "rs", [128, SUB], F32)
            r0 = self.sb(st, "r0", [64, SUB], F32)
            oj = [self.sb(st, "oj%d" % i, [64, qg], F32) for i in range(2)]
            osq = self.sb(st, "osq", [64, SUB], BF16)
            orr = self.sb(st, "orr", [64, SUB], F32)
            ot = self.sb(st, "ot", [64, SUB], F32)
            ost = [self.sb(st, "ost%d" % i, [64, qg], BF16) for i in range(2)]
            ut = self.sb(st, "ut", [128, 2, qg + 2], BF16)
            abt = self.sb(st, "abt", [128, 2, qg], BF16)
            cacc = self.sb(st, "cacc", [128, qg], F32)
            cst = self.sb(st, "cst", [128, qg], BF16)
            el = self.sb(st, "el", [128, 2, 16], BF16)
            elf = self.sb(st, "elf", [128, 16], F32)
            em = self.sb(st, "em", [128, 16], F32)
            e1 = self.sb(st, "e1", [128, 2], F32)
            self.dma(dt_[:], self.dtab.ap(), (), ["dtab"])
            self.dma(ac[:], self.alibi_c.ap(), (), ["alc"])
            ktz = [[self.sb(st, "ktz%d_%d" % (pi, i), [128, KB], BF16) for i in range(NKT)] for pi in range(4)]
            for pi in range(4):
                for i in range(NKT):
                    self.memset(ktz[pi][i][:], 0.0, [ktz[pi][i].name])
            for i in range(NKT):
                self.memset(vx[i][:, :, 64:128], 1.0, [vx[i].name])
            self.dma(qT[:], bass.AP(self.qT_s, tq0, [[self.NT, 128], [128 * self.NT, 2], [1, qg]]), ["qT_s"], ["qT"])
            nblk = 0
            nsc = 0
            nbi = 0
            pend = []
            PD = 3
            LA = 2
            for hh in range(4):
                ch, pb0 = hh // 2, (hh % 2) * 64
                blist = [(kci_, qt_) for kci_ in range(nkc) for qt_ in range(nqt)]
                bq_next, bq_use = 0, 0
                for kb in range(0, Sl, KB):
                    pay, kl = self.key_pay(si, kb)
                    ktb, vxb = kt[nblk % NKT], vx[nblk % NKT]
                    nblk += 1
                    kz = [ktz[(hh % 2) * 2 + j][(nblk - 1) % NKT] for j in range(2)]
                    for j in range(2):
                        pbj = pb0 + j * 32
                        self.dma(kz[j][pbj:pbj + 32, :], self.pay_rows(pay, hh * 64 + j * 32, 32, kl, KB), ["pay"], [kz[j].name])
                    t_, o_, L_ = pay
                    vsrc = bass.AP(t_, o_ + 896 * L_ + kl * 256 + hh * 64, [[256, 128], [128 * 256, KB // 128], [1, 64]])
                    self.dma(vxb[:, :, 0:64], vsrc, ["pay"], [vxb.name])
                    for kc in range(KB // 128):
                        kci = kb // 128 + kc
                        for qt in range(nqt):
                            while bq_next < len(blist) and bq_next <= bq_use + LA:
                                bkci, bqt = blist[bq_next]
                                tbb, abb_ = tb[bq_next % len(tb)], ab_[bq_next % len(ab_)]
                                col = s["acol"] + ((g0 // SUB) + bqt) * nkc + bkci
                                self.tt(tbb[:], dt_[:], ac[:, col:col + 1].to_broadcast([128, SUB]), ALU.add, ["dtab", "alc"], [tbb.name], eng="pool")
                                if (bq_next % 5) < 3:
                                    self.act(abb_[:], tbb[:], AF.Abs, [tbb.name], [abb_.name])
                                else:
                                    self.stt(abb_[:], tbb[:], -1.0, tbb[:], ALU.mult, ALU.min, [tbb.name], [abb_.name])
                                bq_next += 1
                            abb = ab_[bq_use % len(ab_)]
                            sgn = -1.0 if (bq_use % 5) < 3 else 1.0
                            bq_use += 1
                            for j in range(2):
                                pb = pb0 + j * 32
                                psS = P[4 + nsc % 4]
                                ubb, ptb = ub[nsc % len(ub)], pt[nsc % len(pt)]
                                nsc += 1
                                self.mm(psS[:], kz[j][:, kc * 128:(kc + 1) * 128], qT[:, ch, qt * SUB:(qt + 1) * SUB],
                                        True, True, [kz[j].name, "qT"], [psS.name])
                                self.stt(ubb[:], abb[:], sgn * SLOPES[hh] * math.sqrt(32.0), psS[:], ALU.mult, ALU.add,
                                         [abb.name, psS.name], [ubb.name])
                                self.act(ptb[:], ubb[:], AF.Exp, [ubb.name], [ptb.name], scale=32.0 ** -0.5)
                                acc = P[j * 2 + qt]
                                pend.append((acc, vxb, kc, ptb, kci == 0, kci == nkc - 1))
                                if len(pend) > PD:
                                    self.flush_av(pend, 1)
                self.flush_av(pend, len(pend))
                for j in range(2):
                    for qt in range(nqt):
                        acc = P[j * 2 + qt]
                        qs = slice(qt * SUB, (qt + 1) * SUB)
                        self.S.add("dve", lambda e, o=rs[64:128, :], i_=acc[64:128, :]: e.reciprocal(out=o, in_=i_), [acc.name], ["rs"])
                        self.cp(r0[:], rs[64:128, :], ["rs"], ["r0"])
                        self.tt(oj[j][:, qs], acc[0:64, :], r0[:], ALU.mult, [acc.name, "r0"], [oj[j].name])
                o0, o1 = oj
                osb = ost[hh % 2]
                self.stt(o0[:], o1[:], self.nlam[0:64, l:l + 1], o0[:], ALU.mult, ALU.add, [o0.name, o1.name, "nlam"], [o0.name])
                for qt in range(nqt):
                    qs = slice(qt * SUB, (qt + 1) * SUB)
                    self.act(osq[:], o0[:, qs], AF.Square, [o0.name], ["osq"])
                    self.mm(P[7][0:64, :], self.ones[0:64, 0:64], osq[:], True, True, ["osq", "cm"], ["ps7"])
                    self.rsqrt_from_sum(orr[:], P[7][0:64, :], 64, ["ps7"], ["orr"])
                    self.stt(ot[:], o0[:, qs], 1.0 - lam_init, orr[:], ALU.mult, ALU.mult, [o0.name, "orr"], ["ot"])
                    self.act(osb[:, qs], ot[:], AF.Copy, ["ot", "vecs"], [osb.name], scale=self.vcol(l, 5, 0, 64))
                dst = bass.AP(self.ymix, (256 + hh * 64) * self.NT + tq0, [[self.NT, 64], [1, qg]])
                self.dma(dst, osb[:], [osb.name], ["ymix"], eng="pool")
            for hh in range(4):
                qmb = qm[hh % 2]
                self.dma(qmb[:], bass.AP(self.qm_s, hh * 96 * self.NT + tq0, [[self.NT, 96], [1, qg]]), ["qm_s"], [qmb.name])
                for kb in range(0, Sl, KB):
                    pay, kl = self.key_pay(si, kb)
                    ktb, vxb = kt[nblk % NKT], vx[nblk % NKT]
                    nblk += 1
                    self.dma(ktb[0:96, :], self.pay_rows(pay, 256 + hh * 96, 96, kl, KB), ["pay"], [ktb.name])
                    t_, o_, L_ = pay
                    vsrc = bass.AP(t_, o_ + (896 + 256) * L_ + kl * 256 + hh * 64, [[256, 128], [128 * 256, KB // 128], [1, 64]])
                    self.dma(vxb[:, :, 0:64], vsrc, ["pay"], [vxb.name])
                    for kc in range(KB // 128):
                        kci = kb // 128 + kc
                        for qt in range(nqt):
                            psS = P[4 + nsc % 4]
                            ptb = pt[nsc % len(pt)]
                            nsc += 1
                            self.mm(psS[:], ktb[0:96, kc * 128:(kc + 1) * 128], qmb[:, qt * SUB:(qt + 1) * SUB], True, True,
                                    [ktb.name, qmb.name], [psS.name])
                            self.act(ptb[:], psS[:], AF.Exp, [psS.name], [ptb.name], scale=96.0 ** -0.5)
                            acc = P[(hh % 2) * 2 + qt]
                            pend.append((acc, vxb, kc, ptb, kci == 0, kci == nkc - 1))
                            if len(pend) > PD:
                                self.flush_av(pend, 1)
                self.flush_av(pend, len(pend))
                osb = ost[hh % 2]
                for qt in range(nqt):
                    acc = P[(hh % 2) * 2 + qt]
                    qs = slice(qt * SUB, (qt + 1) * SUB)
                    self.S.add("dve", lambda e, o=rs[64:128, :], i_=acc[64:128, :]: e.reciprocal(out=o, in_=i_), [acc.name], ["rs"])
                    self.cp(r0[:], rs[64:128, :], ["rs"], ["r0"])
                    self.tt(osb[:, qs], acc[0:64, :], r0[:], ALU.mult, [acc.name, "r0"], [osb.name])
                dst = bass.AP(self.ymix, (512 + hh * 64) * self.NT + tq0, [[self.NT, 64], [1, qg]])
                self.dma(dst, osb[:], [osb.name], ["ymix"], eng="pool")
            lpay = self.loc_pay(si)
            tl0 = g0
            lo = max(tl0 - 1, 0)
            hi = min(tl0 + qg + 1, s["lq"])
            t_, o_, L_ = lpay
            src = bass.AP(t_, o_ + 640 * L_ + lo, [[L_, 128], [128 * L_, 2], [1, hi - lo]])
            d0 = lo - (tl0 - 1)
            self.dma(ut[:, :, d0:d0 + hi - lo], src, ["pay"], ["ut"])
            self.dma(abt[:], bass.AP(self.ab_s, tq0, [[self.NT, 128], [128 * self.NT, 2], [1, qg]]), ["ab_s"], ["abt"])
            for side, col, srccol in ((0, 0, self.TP - 1), (1, qg + 1, 0)):
                need = (tl0 == 0) if side == 0 else (tl0 + qg == s["lq"])
                if not need:
                    continue
                if not s["prompt"]:
                    self.memset(ut[:, :, col:col + 1], 0.0, ["ut"])
                    continue
                for c in range(2):
                    src = bass.AP(self.gout, (640 + c * 128) * self.TP + srccol, [[self.TP, 128], [PROWS * self.TP, NCORES], [1, 1]])
                    self.dma(el[:, c, 0:8].rearrange("p (r o) -> p r o", o=1), src, ["pay"], ["el"], slow=True)
                self.dma(em[:], self.emask.ap(), (), ["em"])
                for c in range(2):
                    self.cp(elf[:, 0:8], el[:, c, 0:8], ["el"], ["elf"])
                    self.tt(elf[:, 0:8], elf[:, 0:8], em[:, side * 8:side * 8 + 8], ALU.mult, ["elf", "em"], ["elf"])
                    self.S.add("dve", lambda e, o=e1[:, c:c + 1], i_=elf[:, 0:8]: e.reduce_sum(out=o, in_=i_, axis=AX.X), ["elf"], ["e1"])
                    self.cp(ut[:, c, col:col + 1], e1[:, c:c + 1], ["e1"], ["ut"])
            for c in range(2):
                self.ts(cacc[:], ut[:, c, 0:qg], self.vcol(l, 10 + c * 3), None, ALU.mult, None, ["ut", "vecs"], ["cacc"])
                self.stt(cacc[:], ut[:, c, 1:qg + 1], self.vcol(l, 11 + c * 3), cacc[:], ALU.mult, ALU.add, ["ut", "vecs", "cacc"], ["cacc"])
                self.stt(cacc[:], ut[:, c, 2:qg + 2], self.vcol(l, 12 + c * 3), cacc[:], ALU.mult, ALU.add, ["ut", "vecs", "cacc"], ["cacc"])
                self.tt(cst[:], cacc[:], abt[:, c, :], ALU.mult, ["cacc", "abt"], ["cst"])
                dst = bass.AP(self.ymix, (c * 128) * self.NT + tq0, [[self.NT, 128], [1, qg]])
                self.dma(dst, cst[:], ["cst"], ["ymix"], eng="pool")


def _bf(a):
    return np.ascontiguousarray(a.astype(ml_dtypes.bfloat16))


def _consts(B, core):
    TP, SS, NT = B.TP, B.SS, B.NT
    pos = np.concatenate([core * TP + np.arange(TP), np.arange(SS), np.arange(SS)]).astype(np.float32)
    inv = (10000.0 ** (-np.arange(0, 32, 2, dtype=np.float32) / 32)).astype(np.float32)
    ang = (pos[None, :] * np.tile(inv, 2)[:, None]).astype(np.float32)
    ropeC = np.cos(ang.astype(np.float64)).astype(np.float32)
    sgn = np.concatenate([-np.ones(16), np.ones(16)])[:, None]
    ropeS = (np.sin(ang.astype(np.float64)) * sgn).astype(np.float32)
    al = np.zeros((B.NACOL,), np.float32)
    for s in B.seqs:
        nqt, nkc = s["lq"] // SUB, s["S"] // 128
        base = core * TP if s["prompt"] else 0
        for qt in range(nqt):
            for kc in range(nkc):
                al[s["acol"] + qt * nkc + kc] = base + qt * SUB - kc * 128
    alibi_c = np.tile(al[None, :], (128, 1)).astype(np.float32)
    dtab = (np.arange(SUB)[None, :] - np.arange(128)[:, None]).astype(np.float32)
    ones = np.ones((128, 128))
    blk32 = np.kron(np.eye(4), np.ones((32, 32)))
    blk64 = np.kron(np.eye(2), np.ones((64, 64)))
    a64 = 2 * np.pi * np.outer(np.arange(64), np.arange(64)) / 64
    c64 = np.kron(np.eye(2), np.cos(a64))
    s64 = np.kron(np.eye(2), np.sin(a64))
    cm = _bf(np.concatenate([ones, blk32, blk64, c64, s64], 1))
    out = dict(ropeC=ropeC, ropeS=ropeS, alibi_c=alibi_c, dtab=dtab, cmats=cm)
    for tag, n1, Sl in (("p", B.N1P, B.SP), ("s", B.N1S, B.SS)):
        a1 = 2 * np.pi * np.outer(np.arange(n1), np.arange(n1)) / n1
        C1, S1 = np.cos(a1), np.sin(a1)
        out["f1" + tag] = _bf(np.concatenate([C1, S1, -S1, C1], 1))
        at = 2 * np.pi * np.outer(np.arange(128), np.arange(n1)) / Sl
        nrm = 1.0 / math.sqrt(Sl * 64.0)
        out["tw" + tag] = np.concatenate([np.cos(at) * nrm, np.sin(at) * nrm], 1).astype(np.float32)
    a3 = 2 * np.pi * np.outer(np.arange(128), np.arange(128)) / 128
    C3, S3 = np.cos(a3), np.sin(a3)
    k0 = slice(core * B.NK0P, (core + 1) * B.NK0P)
    out["f3p"] = _bf(np.concatenate([C3[:, k0], -S3[:, k0]], 1))
    out["f3s"] = _bf(np.concatenate([C3, -S3], 1))
    em = np.zeros((128, 16), np.float32)
    if core > 0:
        em[:, core - 1] = 1.0
    if core < NCORES - 1:
        em[:, 8 + core + 1] = 1.0
    out["emask"] = em
    return out


def _layout_weights(inp):
    f = lambda k: np.asarray(inp[k], np.float32)
    out = {}
    out["w_mod"] = f("w_mod")
    out["b_mod_r"] = np.ascontiguousarray(f("b_mod").reshape(DEPTH, 72, 128).transpose(0, 2, 1))
    out["norm_g_r"] = np.ascontiguousarray(f("norm_g").reshape(DEPTH, 3, KC, 128).transpose(0, 3, 1, 2).reshape(DEPTH, 128, 24))
    gu = np.stack([f("ffn1_w_gu"), f("ffn2_w_gu")], 1)
    gu = gu.reshape(DEPTH, 2, KC, 128, 2, FJ, 128)
    out["wgu_r"] = np.ascontiguousarray(gu.transpose(0, 1, 5, 3, 2, 4, 6).reshape(DEPTH * 2 * FJ, 128, KC * 256))
    dn = np.stack([f("ffn1_w_down"), f("ffn2_w_down")], 1)
    dn = dn.reshape(DEPTH, 2, FJ, 128, KC, 128)
    out["wdn_r"] = np.ascontiguousarray(dn.transpose(0, 1, 4, 3, 2, 5).reshape(DEPTH * 2 * KC, 128, FJ * 128))
    win = f("w_in")
    out["w_in_r"] = np.ascontiguousarray(win.reshape(DEPTH, KC, 128, INC).transpose(0, 2, 1, 3).reshape(DEPTH, 128, KC * INC))
    out["w_out_r"] = np.ascontiguousarray(f("w_out").reshape(DEPTH, KC, 128, D).transpose(0, 2, 1, 3).reshape(DEPTH, 128, KC * D))
    uq = f("mla_w_uq").reshape(DEPTH, 2, 128, 4, 96)
    sw = np.concatenate([uq[..., 0:64], uq[..., 80:96], uq[..., 64:80]], -1)
    out["wuq_r"] = np.ascontiguousarray(np.stack([uq, sw], 3).transpose(0, 2, 1, 3, 4, 5).reshape(DEPTH, 128, 2 * 2 * 384))
    ukv = f("mla_w_ukv").reshape(DEPTH, 128, 4, 128)
    ukn = np.concatenate([ukv[..., 0:64], np.zeros_like(ukv[..., 0:32])], -1)
    out["wukn_r"] = np.ascontiguousarray(ukn.reshape(DEPTH, 128, 4 * 96))
    out["wukv_r"] = np.ascontiguousarray(ukv[..., 64:128].reshape(DEPTH, 128, 256))
    kpe = win[:, :, C_KPE:C_KPE + 32].reshape(DEPTH, KC, 128, 32)
    z = np.zeros_like(kpe[..., 0:64].repeat(1, -1)) if False else np.zeros(kpe.shape[:-1] + (64,), np.float32)
    pl = np.concatenate([z, kpe], -1)
    sp = np.concatenate([z, kpe[..., 16:32], kpe[..., 0:16]], -1)
    out["wkpe_r"] = np.ascontiguousarray(np.stack([pl, sp], 3).transpose(0, 2, 1, 3, 4).reshape(DEPTH, 128, KC * 2 * 96))
    vec = np.zeros((DEPTH, 128, 16), np.float32)
    for l in range(DEPTH):
        lam_init = 0.8 - 0.6 * math.exp(-0.3 * l)
        vec[l, :, 0] = np.tile(f("diff_q_g")[l], 4)
        vec[l, :, 1] = np.tile(f("diff_k_g")[l], 4)
        vec[l, :, 2] = f("mla_q_a_g")[l][0:128]
        vec[l, :, 3] = f("mla_q_a_g")[l][128:256]
        vec[l, :, 4] = f("mla_kv_a_g")[l]
        vec[l, :, 5] = np.tile(f("diff_subln_g")[l], 2)
        for c, key in ((6, "mla_q_g"), (8, "mla_k_g")):
            g = f(key)[l]
            vec[l, 0:96, c] = g
            vec[l, 0:96, c + 1] = np.concatenate([g[0:64], g[80:96], g[64:80]])
        cw = f("conv_w")[l]
        for c in range(2):
            for t in range(3):
                vec[l, :, 10 + c * 3 + t] = cw[t, c * 128:(c + 1) * 128]
    out["vecs_r"] = vec
    out["lam_r"] = np.ascontiguousarray(np.tile(f("diff_lambda").reshape(DEPTH, 1, 128), (1, 128, 1)))
    return out


_CACHE = {}
SUBLN_SCALE_IN_VEC = True


def _get_builder(sp_len, ss_len):
    key = (sp_len, ss_len)
    if key not in _CACHE:
        B = Builder(sp_len, ss_len)
        B.build()
        _CACHE[key] = B
    return _CACHE[key]


def kernel(**inp):
    xp = np.asarray(inp["x_prompt"], np.float32)
    xs = np.asarray(inp["x_sample"], np.float32)
    cp = np.asarray(inp["c_prompt"], np.float32)
    cs = np.asarray(inp["c_sample"], np.float32)
    sp_len, ss_len = xp.shape[1], xs.shape[1]
    B = _get_builder(sp_len, ss_len)
    TP = B.TP
    W = _layout_weights(inp)
    in_maps = []
    for core in range(NCORES):
        m = dict(W)
        xt = np.concatenate([xp[0, core * TP:(core + 1) * TP], xs[2 * core], xs[2 * core + 1]], 0)
        m["xT"] = np.ascontiguousarray(xt.T)
        c3 = np.stack([cp[0], cs[2 * core], cs[2 * core + 1]], 1)
        m["cT"] = np.ascontiguousarray(c3.reshape(KC, 128, 3).transpose(1, 0, 2).reshape(128, KC * 3))
        m.update(_consts(B, core))
        in_maps.append(m)
    res = run_bass_kernel_spmd(B.nc, in_maps, core_ids=list(range(NCORES)))
    yp = np.zeros_like(xp)
    ys = np.zeros_like(xs)
    for core in range(NCORES):
        yt = np.asarray(res.results[core]["yT"]).T
        yp[0, core * TP:(core + 1) * TP] = yt[0:TP]
        ys[2 * core] = yt[TP:TP + ss_len]
        ys[2 * core + 1] = yt[TP + ss_len:]
    return (yp, ys)
```
